# Optimizing a Trainium2 kernel written in Bass

```python
import jax, jax.numpy as jnp
from jax import lax
import numpy as np

D_MODEL = 4096
BATCH = 1
SEQ = 16384
DEPTH = 1
DEC_BATCH = 8
DEC_SEQ = 16
PAST_LEN = 2048

CHUNK = 64
HEAD_DIM = 128
GDN_QK_HEADS = 8
GDN_V_HEADS = 16
GDN_QK_WIDTH = GDN_QK_HEADS * HEAD_DIM
GDN_V_WIDTH = GDN_V_HEADS * HEAD_DIM
CONV_WIDTH = 4
CONV_CH = 2 * GDN_QK_WIDTH + GDN_V_WIDTH
SB_HEADS = 16
SB_WIDTH = SB_HEADS * HEAD_DIM
SB_BLOCK = 128
SB_SEG = 2048
D_FF = 4 * D_MODEL
EPS = 1e-6
FAR_POS = 2 ** 30

OFF_Z = CONV_CH
OFF_A = OFF_Z + GDN_V_WIDTH
OFF_B = OFF_A + GDN_V_HEADS
OFF_SB = OFF_B + GDN_V_HEADS
OFF_GATE = OFF_SB + 3 * SB_WIDTH
IN_WIDTH = OFF_GATE + 2 * D_MODEL

kernel_name = "hybrid_gdn_stickbreaking_stream_step"


def rmsnorm(x, w):
    xf = x.astype(jnp.float32)
    var = jnp.mean(xf * xf, axis=-1, keepdims=True)
    return (xf * lax.rsqrt(var + EPS) * w.astype(jnp.float32)).astype(x.dtype)


def l2norm(t):
    return t * lax.rsqrt(jnp.sum(t * t, axis=-1, keepdims=True) + EPS)


def causal_conv_silu(x, past, w):
    L = x.shape[1]
    xp = jnp.concatenate([past.astype(x.dtype), x], axis=1)
    out = w[0] * xp[:, 0:L]
    for i in range(1, CONV_WIDTH):
        out = out + w[i] * xp[:, i:i + L]
    return jax.nn.silu(out), xp[:, -(CONV_WIDTH - 1):]


def _to_blocks(t, chunk):
    B, L, H = t.shape[:3]
    t = t.reshape((B, L // chunk, chunk, H) + t.shape[3:])
    return jnp.moveaxis(t, (1, 3), (0, 2))


def gated_delta_rule(q, k, v, g, beta, S0, chunk):
    B, L, H, dk = k.shape
    dv = v.shape[-1]
    qc, kc, vc = _to_blocks(q, chunk), _to_blocks(k, chunk), _to_blocks(v, chunk)
    gc = jnp.cumsum(_to_blocks(g, chunk), axis=-1)
    bc = _to_blocks(beta, chunk)
    idx = jnp.arange(chunk)
    causal = idx[:, None] >= idx[None, :]
    strict = idx[:, None] > idx[None, :]
    diff = gc[..., :, None] - gc[..., None, :]
    decay = jnp.where(causal, jnp.exp(jnp.where(causal, diff, 0.0)), 0.0)
    kk = jnp.einsum('nbhid,nbhjd->nbhij', kc, kc)
    tri = jnp.eye(chunk, dtype=jnp.float32) + jnp.where(strict, bc[..., :, None] * kk * decay, 0.0)
    rhs = jnp.concatenate([vc * bc[..., None], kc * (bc * jnp.exp(gc))[..., None]], axis=-1)
    sol = lax.linalg.triangular_solve(tri, rhs, left_side=True, lower=True, unit_diagonal=True)
    u0, w = sol[..., :dv], sol[..., dv:]
    aqk = jnp.einsum('nbhid,nbhjd->nbhij', qc, kc) * decay
    qg = qc * jnp.exp(gc)[..., None]
    g_last = gc[..., -1]
    kd = kc * jnp.exp(g_last[..., None] - gc)[..., None]

    def step(S, xs):
        u0_, w_, aqk_, qg_, kd_, gl_ = xs
        u = u0_ - jnp.einsum('bhcd,bhde->bhce', w_, S)
        o = jnp.einsum('bhcd,bhde->bhce', qg_, S) + jnp.einsum('bhij,bhje->bhie', aqk_, u)
        S = jnp.exp(gl_)[..., None, None] * S + jnp.einsum('bhcd,bhce->bhde', kd_, u)
        return S, o

    S, o = lax.scan(step, S0, (u0, w, aqk, qg, kd, g_last))
    o = jnp.moveaxis(o, (0, 2), (1, 3)).reshape(B, L, H, dv)
    return o, S


def stick_breaking(q, k, v, q_pos, k_pos):
    B, K, H, d = k.shape
    pad = (-K) % SB_BLOCK
    if pad:
        k = jnp.pad(k, ((0, 0), (0, pad), (0, 0), (0, 0)))
        v = jnp.pad(v, ((0, 0), (0, pad), (0, 0), (0, 0)))
        k_pos = jnp.concatenate([k_pos, jnp.full((pad,), FAR_POS, jnp.int32)])
    nk = (K + pad) // SB_BLOCK
    kb = k.reshape(B, nk, SB_BLOCK, H, d).astype(jnp.float32)
    vb = v.reshape(B, nk, SB_BLOCK, H, d).astype(jnp.float32)
    z = jnp.einsum('bqhd,bnkhd->bhqnk', q.astype(jnp.float32), kb) * (HEAD_DIM ** -0.5)
    mask = k_pos.reshape(nk, SB_BLOCK)[None] < q_pos[:, None, None]
    log_fail = jnp.where(mask, jax.nn.log_sigmoid(-z), 0.0)
    idx = jnp.arange(SB_BLOCK)
    upper = (idx[:, None] >= idx[None, :]).astype(jnp.float32)
    within = jnp.einsum('bhqnj,js->bhqns', log_fail, upper)
    totals = jnp.sum(log_fail, axis=-1)
    later = lax.cumsum(totals, axis=3, reverse=True) - totals
    tail = within + later[..., None] - log_fail
    att = jnp.where(mask, jnp.exp(jax.nn.log_sigmoid(z) + tail), 0.0)
    return jnp.einsum('bhqnk,bnkhd->bqhd', att, vb)


def stick_breaking_prompt(q, k, v):
    B, L, H, d = q.shape
    outs = []
    for s0 in range(0, L, SB_SEG):
        ls = min(SB_SEG, L - s0)
        nb = ls // SB_BLOCK
        kk, vv = k[:, :s0 + ls], v[:, :s0 + ls]
        k_pos = jnp.arange(s0 + ls, dtype=jnp.int32)
        qb = jnp.moveaxis(q[:, s0:s0 + ls].reshape(B, nb, SB_BLOCK, H, d), 1, 0)

        def one_block(args, kk=kk, vv=vv, k_pos=k_pos, s0=s0):
            q_blk, i = args
            q_pos = s0 + i * SB_BLOCK + jnp.arange(SB_BLOCK, dtype=jnp.int32)
            return stick_breaking(q_blk, kk, vv, q_pos, k_pos)

        o = lax.map(one_block, (qb, jnp.arange(nb, dtype=jnp.int32)))
        outs.append(jnp.moveaxis(o, 0, 1).reshape(B, ls, H * d))
    return jnp.concatenate(outs, axis=1)


def layer(x, conv_past, S0, past_k, past_v, chunk,
          norm1_w, w_in, conv_w, A_log, dt_bias, gdn_norm_w, w_gdn_o, w_sb_o, w_out,
          norm2_w, w_up, w_down):
    B, L, _ = x.shape
    xn = rmsnorm(x, norm1_w)
    proj = xn @ w_in
    qkv_raw, z, a, b, sb_qkv, gates = jnp.split(proj, [OFF_Z, OFF_A, OFF_B, OFF_SB, OFF_GATE], axis=-1)

    qkv, conv_state = causal_conv_silu(qkv_raw, conv_past, conv_w)
    qkv = qkv.astype(jnp.float32)
    q, k, v = jnp.split(qkv, [GDN_QK_WIDTH, 2 * GDN_QK_WIDTH], axis=-1)
    rep = GDN_V_HEADS // GDN_QK_HEADS
    q = jnp.repeat(l2norm(q.reshape(B, L, GDN_QK_HEADS, HEAD_DIM)) * (HEAD_DIM ** -0.5), rep, axis=2)
    k = jnp.repeat(l2norm(k.reshape(B, L, GDN_QK_HEADS, HEAD_DIM)), rep, axis=2)
    v = v.reshape(B, L, GDN_V_HEADS, HEAD_DIM)
    g = -jnp.exp(A_log.astype(jnp.float32)) * jax.nn.softplus(a.astype(jnp.float32) + dt_bias.astype(jnp.float32))
    beta = jax.nn.sigmoid(b.astype(jnp.float32))
    o_a, S = gated_delta_rule(q, k, v, g, beta, S0.astype(jnp.float32), chunk)
    o_a = rmsnorm(o_a, gdn_norm_w) * jax.nn.silu(z.astype(jnp.float32).reshape(B, L, GDN_V_HEADS, HEAD_DIM))
    y_a = o_a.reshape(B, L, GDN_V_WIDTH).astype(x.dtype) @ w_gdn_o

    q_sb, k_sb, v_sb = [t.reshape(B, L, SB_HEADS, HEAD_DIM) for t in jnp.split(sb_qkv, 3, axis=-1)]
    if past_k is None:
        o_b = stick_breaking_prompt(q_sb, k_sb, v_sb)
    else:
        P = past_k.shape[1]
        keys = jnp.concatenate([past_k.astype(x.dtype), k_sb], axis=1)
        vals = jnp.concatenate([past_v.astype(x.dtype), v_sb], axis=1)
        q_pos = P + jnp.arange(L, dtype=jnp.int32)
        k_pos = jnp.arange(P + L, dtype=jnp.int32)
        o_b = stick_breaking(q_sb, keys, vals, q_pos, k_pos).reshape(B, L, SB_WIDTH)
    y_b = o_b.astype(x.dtype) @ w_sb_o

    g_a, g_b = jnp.split(jax.nn.sigmoid(gates), 2, axis=-1)
    h = x + (g_a * y_a + g_b * y_b) @ w_out

    h = h + jnp.square(jax.nn.relu(rmsnorm(h, norm2_w) @ w_up)) @ w_down
    return h, conv_state, S.astype(S0.dtype), k_sb, v_sb


def setup_inputs(seed: int = 0) -> dict:
    key = jax.random.key(seed)
    ks = jax.random.split(key, 24)

    def nrm(k, shape, scale):
        return jax.random.normal(k, shape, jnp.float32) * scale

    dt = jnp.exp(jax.random.uniform(ks[9], (DEPTH, GDN_V_HEADS), jnp.float32,
                                    np.log(1e-3).astype(np.float32), np.log(1e-1).astype(np.float32)))
    return {
        "x_prompt": nrm(ks[0], (BATCH, SEQ, D_MODEL), 1.0),
        "x_sample": nrm(ks[1], (DEC_BATCH, DEC_SEQ, D_MODEL), 1.0),
        "cache_sb_k": nrm(ks[2], (DEPTH, DEC_BATCH, PAST_LEN, SB_HEADS, HEAD_DIM), 1.0),
        "cache_sb_v": nrm(ks[3], (DEPTH, DEC_BATCH, PAST_LEN, SB_HEADS, HEAD_DIM), 1.0),
        "state_gdn_S": nrm(ks[4], (DEPTH, DEC_BATCH, GDN_V_HEADS, HEAD_DIM, HEAD_DIM), 0.1),
        "state_gdn_conv": nrm(ks[5], (DEPTH, DEC_BATCH, CONV_WIDTH - 1, CONV_CH), 1.0),
        "norm1_w": 1.0 + nrm(ks[6], (DEPTH, D_MODEL), 0.01),
        "w_in": nrm(ks[7], (DEPTH, D_MODEL, IN_WIDTH), D_MODEL ** -0.5),
        "conv_w": nrm(ks[8], (DEPTH, CONV_WIDTH, CONV_CH), CONV_WIDTH ** -0.5),
        "A_log": jnp.log(jax.random.uniform(ks[10], (DEPTH, GDN_V_HEADS), jnp.float32, 1.0, 16.0)),
        "dt_bias": dt + jnp.log(-jnp.expm1(-dt)),
        "gdn_norm_w": 1.0 + nrm(ks[11], (DEPTH, HEAD_DIM), 0.01),
        "w_gdn_o": nrm(ks[12], (DEPTH, GDN_V_WIDTH, D_MODEL), GDN_V_WIDTH ** -0.5),
        "w_sb_o": nrm(ks[13], (DEPTH, SB_WIDTH, D_MODEL), SB_WIDTH ** -0.5),
        "w_out": nrm(ks[14], (DEPTH, D_MODEL, D_MODEL), D_MODEL ** -0.5),
        "norm2_w": 1.0 + nrm(ks[15], (DEPTH, D_MODEL), 0.01),
        "w_up": nrm(ks[16], (DEPTH, D_MODEL, D_FF), D_MODEL ** -0.5),
        "w_down": nrm(ks[17], (DEPTH, D_FF, D_MODEL), D_FF ** -0.5),
        "final_norm_w": 1.0 + nrm(ks[18], (D_MODEL,), 0.01),
    }


def reference(x_prompt, x_sample, cache_sb_k, cache_sb_v, state_gdn_S, state_gdn_conv,
              norm1_w, w_in, conv_w, A_log, dt_bias, gdn_norm_w, w_gdn_o, w_sb_o, w_out,
              norm2_w, w_up, w_down, final_norm_w):
    hp, hs = x_prompt, x_sample
    Bp = x_prompt.shape[0]
    pk, pv, pS, pc, sk, sv, sS, sc = [], [], [], [], [], [], [], []
    for l in range(DEPTH):
        weights = (norm1_w[l], w_in[l], conv_w[l], A_log[l], dt_bias[l], gdn_norm_w[l],
                   w_gdn_o[l], w_sb_o[l], w_out[l], norm2_w[l], w_up[l], w_down[l])
        conv0 = jnp.zeros((Bp, CONV_WIDTH - 1, CONV_CH), hp.dtype)
        S0 = jnp.zeros((Bp, GDN_V_HEADS, HEAD_DIM, HEAD_DIM), state_gdn_S.dtype)
        hp, c_p, S_p, k_p, v_p = layer(hp, conv0, S0, None, None, CHUNK, *weights)
        hs, c_s, S_s, k_s, v_s = layer(hs, state_gdn_conv[l], state_gdn_S[l], cache_sb_k[l], cache_sb_v[l],
                                       hs.shape[1], *weights)
        pk.append(k_p); pv.append(v_p); pS.append(S_p); pc.append(c_p)
        sk.append(k_s); sv.append(v_s); sS.append(S_s); sc.append(c_s)
    y_prompt = rmsnorm(hp, final_norm_w)
    y_sample = rmsnorm(hs, final_norm_w)
    return (y_prompt, y_sample,
            jnp.stack(pk), jnp.stack(pv), jnp.stack(pS), jnp.stack(pc),
            jnp.stack(sk), jnp.stack(sv), jnp.stack(sS), jnp.stack(sc))
```

```python
import contextlib
import numpy as np
import ml_dtypes
import concourse.bass as bass
import concourse.mybir as mybir
from concourse.bass_utils import run_bass_kernel_spmd

F32 = mybir.dt.float32
BF16 = mybir.dt.bfloat16
AF = mybir.ActivationFunctionType
ALU = mybir.AluOpType

D = 4096
HD = 128
NKC = 32
DEC_B = 8
DEC_S = 16
PAST = 2048
DFF = 16384
CONV_CH = 4096
OFF_Z = 4096
OFF_A = OFF_Z + 2048
OFF_B = OFF_A + 16
OFF_SB = OFF_B + 16
OFF_GATE = OFF_SB + 3 * 2048
IN_W = OFF_GATE + 2 * D
EPS = 1e-6
W1C = 1540
SAME_ENGINE_SYNC = True


class V:
    __slots__ = ("ap", "key")

    def __init__(self, ap, key):
        self.ap = ap
        self.key = key


class Tl:
    def __init__(self, t, name, is_dram=False):
        self.t = t
        self.name = name
        self.is_dram = is_dram

    def __getitem__(self, idx):
        return V(self.t[idx], self.name)

    def k(self, suffix):
        return _Sub(self, (self.name, suffix))


class _Sub:
    def __init__(self, tl, key):
        self.tl = tl
        self.key = key

    def __getitem__(self, idx):
        return V(self.tl.t[idx], self.key)


class Prog:
    COMPUTE = ("pe", "act", "dve", "pool")

    def __init__(self, nc, es):
        self.nc = nc
        self.eng = {"pe": nc.tensor, "act": nc.scalar, "dve": nc.vector, "pool": nc.gpsimd, "sp": nc.sync}
        self.prog = {e: [] for e in self.eng}
        self.esem = {e: es.enter_context(nc.semaphore("sem_" + e)) for e in self.COMPUTE}
        self.ecnt = {e: 0 for e in self.COMPUTE}
        self.nslots = 12
        self.dslots = {q: [es.enter_context(nc.semaphore("dsem_%s_%d" % (q, i))) for i in range(self.nslots)]
                       for q in ("sp", "poolq")}
        self.dcnt = {q: [0] * self.nslots for q in ("sp", "poolq")}
        self.drr = {"sp": 0, "poolq": 0}
        self.lastw = {}
        self.readers = {}
        self.semobj = {}
        self.bar = {}

    def barrier(self):
        b = {}
        for e in self.COMPUTE:
            if self.ecnt[e]:
                b["E" + e] = (self.ecnt[e], e)
        for q in ("sp", "poolq"):
            for j in range(self.nslots):
                if self.dcnt[q][j]:
                    b["D%s%d" % (q, j)] = (self.dcnt[q][j], "dma_" + q)
        self.bar = b

    def _deps(self, reads, writes):
        deps = dict(self.bar)

        def add(tok):
            sid, val, eng = tok
            if sid not in deps or deps[sid][0] < val:
                deps[sid] = (val, eng)

        for k in reads:
            if k in self.lastw:
                add(self.lastw[k])
        for k in writes:
            if k in self.lastw:
                add(self.lastw[k])
            for sid, (val, eng) in self.readers.get(k, {}).items():
                add((sid, val, eng))
        return deps

    def _record(self, tok, reads, writes):
        sid, val, eng = tok
        for k in writes:
            self.lastw[k] = tok
            self.readers[k] = {}
        for k in reads:
            if k in writes:
                continue
            self.readers.setdefault(k, {})[sid] = (val, eng)

    @staticmethod
    def _is_psum(k):
        if isinstance(k, tuple):
            k = k[0]
        return isinstance(k, str) and (k.startswith("pbig") or k.startswith("psm"))

    def op(self, e, fn, reads, writes):
        extra = [k for k in reads if self._is_psum(k) and k not in writes]
        if extra:
            writes = list(writes) + extra
        deps = self._deps(reads, writes)
        if e in self.COMPUTE:
            waits = []
            for sid, (val, peng) in deps.items():
                if peng == e and (e == "pe" or not SAME_ENGINE_SYNC):
                    continue
                waits.append((sid, val))
            self.ecnt[e] += 1
            sem = self.esem[e]
            sid = "E" + e
            self.semobj[sid] = sem
            tok = (sid, self.ecnt[e], e)
            self.prog[e].append((fn, waits, sem, 1))
        else:
            q = e
            j = self.drr[q]
            self.drr[q] = (j + 1) % self.nslots
            sem = self.dslots[q][j]
            sid = "D%s%d" % (q, j)
            self.semobj[sid] = sem
            waits = [(s, v) for s, (v, _) in deps.items()]
            if self.dcnt[q][j] > 0:
                waits.append((sid, self.dcnt[q][j]))
            self.dcnt[q][j] += 16
            tok = (sid, self.dcnt[q][j], "dma_" + q)
            self.prog["pool" if q == "poolq" else q].append((fn, waits, sem, 16))
        self._record(tok, reads, writes)

    def mm(self, out, lhsT, rhs, start=True, stop=True):
        self.op("pe", lambda en: en.matmul(out.ap, lhsT=lhsT.ap, rhs=rhs.ap, start=start, stop=stop),
                [lhsT.key, rhs.key], [out.key])

    def tr(self, out, in_, ident):
        self.op("pe", lambda en: en.transpose(out.ap, in_.ap, ident.ap), [in_.key, ident.key], [out.key])

    def act(self, out, in_, func, bias=None, scale=None, accum=None):
        kw = {}
        rd = [in_.key]
        wr = [out.key]
        if bias is not None:
            if isinstance(bias, V):
                kw["bias"] = bias.ap
                rd.append(bias.key)
            else:
                kw["bias"] = bias
        if scale is not None:
            if isinstance(scale, V):
                kw["scale"] = scale.ap
                rd.append(scale.key)
            else:
                kw["scale"] = scale
        if accum is not None:
            kw["accum_out"] = accum.ap
            wr.append(accum.key)
        self.op("act", lambda en: en.activation(out=out.ap, in_=in_.ap, func=func, **kw), rd, wr)

    def tt(self, e, out, in0, in1, op):
        self.op(e, lambda en: en.tensor_tensor(out=out.ap, in0=in0.ap, in1=in1.ap, op=op),
                [in0.key, in1.key], [out.key])

    def ts(self, e, out, in0, s1, op0, s2=None, op1=None):
        if e == "act_ts":
            self.op("act", lambda en: en.activation(out=out.ap, in_=in0.ap, func=AF.Copy, scale=s1.ap),
                    [in0.key, s1.key], [out.key])
            return
        rd = [in0.key]
        a1 = s1
        if isinstance(s1, V):
            a1 = s1.ap
            rd.append(s1.key)
        a2 = s2
        if isinstance(s2, V):
            a2 = s2.ap
            rd.append(s2.key)
        if op1 is None:
            self.op(e, lambda en: en.tensor_scalar(out=out.ap, in0=in0.ap, scalar1=a1, scalar2=None, op0=op0),
                    rd, [out.key])
        else:
            self.op(e, lambda en: en.tensor_scalar(out=out.ap, in0=in0.ap, scalar1=a1, scalar2=a2, op0=op0, op1=op1),
                    rd, [out.key])

    def stt(self, e, out, in0, scalar, in1, op0, op1):
        rd = [in0.key, in1.key]
        sc = scalar
        if isinstance(scalar, V):
            sc = scalar.ap
            rd.append(scalar.key)
        self.op(e, lambda en: en.scalar_tensor_tensor(out=out.ap, in0=in0.ap, scalar=sc, in1=in1.ap, op0=op0, op1=op1),
                rd, [out.key])

    def copy(self, e, out, in_):
        if e == "act":
            self.op(e, lambda en: en.activation(out=out.ap, in_=in_.ap, func=AF.Copy), [in_.key], [out.key])
        else:
            self.op(e, lambda en: en.tensor_copy(out.ap, in_.ap), [in_.key], [out.key])

    def rsqrt(self, out, in_):
        self.act(out, in_, AF.Ln)
        self.act(out, out, AF.Exp, scale=-0.5)

    def recip(self, out, in_):
        self.op("dve", lambda en: en.reciprocal(out.ap, in_.ap), [in_.key], [out.key])

    def memset(self, e, out, val):
        self.op(e, lambda en: en.memset(out.ap, val), [], [out.key])

    def dma(self, q, out, in_, **kw):
        self.op(q, lambda en: en.dma_start(out=out.ap, in_=in_.ap, **kw), [in_.key], [out.key])

    def emit(self):
        nc = self.nc
        finals = []
        for e in self.COMPUTE:
            if self.ecnt[e]:
                finals.append((self.esem[e], self.ecnt[e]))
        for q in ("sp", "poolq"):
            for j in range(self.nslots):
                if self.dcnt[q][j]:
                    finals.append((self.dslots[q][j], self.dcnt[q][j]))
        with nc.Block() as block:
            def run(e):
                def body(en):
                    waited = {}
                    for fn, waits, sem, inc in self.prog[e]:
                        for sid, val in waits:
                            if waited.get(sid, 0) < val:
                                en.wait_ge(self.semobj[sid], val)
                                waited[sid] = val
                        fn(en).then_inc(sem, inc)
                    if e == "sp":
                        for sem, val in finals:
                            en.wait_ge(sem, val)
                return body
            block.sync(run("sp"))
            block.tensor(run("pe"))
            block.scalar(run("act"))
            block.vector(run("dve"))
            block.gpsimd(run("pool"))


class RR:
    def __init__(self, items):
        self.items = items
        self.i = 0

    def next(self):
        x = self.items[self.i % len(self.items)]
        self.i += 1
        return x


def host_consts():
    i = np.arange(128)
    blk = i // 16
    sameb = (blk[:, None] == blk[None, :]).astype(np.float32)
    ident = np.eye(128, dtype=np.float32)
    ones = np.ones((128, 128), np.float32)
    UT = (i[:, None] <= i[None, :]).astype(np.float32)
    SL = (i[:, None] > i[None, :]).astype(np.float32)
    SLT = SL.T.copy()
    rowm = np.zeros((128, 128), np.float32)
    for b in range(8):
        rowm[16 * b:16 * b + 16, b] = 1.0
    cf = np.stack([ident, ones, UT, SL, SLT, UT * sameb, SL * sameb, SLT * sameb, rowm], axis=1)
    NU = -(i[:, None] >= i[None, :]).astype(np.float32)
    NO = -ones
    MS = (i[:, None] < i[None, :]).astype(np.float32)
    mg = []
    for r in range(4):
        m = np.zeros((128, 4, 128), np.float32)
        m[:, r, :] = MS
        m[:, r + 1:, :] = 1.0
        mg.append(m.reshape(128, 512))
    cb = np.concatenate([ident, NU, NO, MS] + mg, axis=1).astype(ml_dtypes.bfloat16)
    return np.ascontiguousarray(cf), np.ascontiguousarray(cb)


CF_ID, CF_ONES, CF_UT, CF_SL, CF_SLT, CF_UTB, CF_SLB, CF_SLTB, CF_ROWM = range(9)


import os
P1LIM = int(os.environ.get('P1LIM', '9'))
P1SUB = os.environ.get('P1SUB', 'z')
GT = 256


def build_phase1(SEQ, groups, fused):
    nc = bass.Bass("TRN2", target_bir_lowering=False)
    NTP = SEQ // 128
    NGP = SEQ // GT
    NTG = GT // 128
    NTOK = SEQ + 128
    NG = NGP + 1
    NQG = SEQ // 512
    dt = nc.dram_tensor

    def din(name, shape, dtype=F32):
        return dt(name, list(shape), dtype, kind="ExternalInput").ap()

    def dout(name, shape, dtype=F32):
        return dt(name, list(shape), dtype, kind="ExternalOutput").ap()

    x_prompt = din("x_prompt", [SEQ, D])
    x_sample = din("x_sample", [128, D])
    cf_in = din("cf", [128, 9, 128])
    cb_in = din("cb", [128, 2560], BF16)
    norm1_w = din("norm1_w", [1, D])
    gdn_norm_w = din("gdn_norm_w", [1, 128])
    ngr = len(groups)
    w1 = din("w1", [ngr, D, W1C])
    convw = din("convw", [ngr, 4, 4, 128])
    alog = din("alog", [ngr, 1, 2])
    dtb = din("dtb", [ngr, 1, 2])
    cache_k = din("cache_k", [ngr, 8, PAST, 2, 128])
    cache_v = din("cache_v", [ngr, 8, PAST, 2, 128])
    st_S = din("st_S", [ngr, 8, 2, 128, 128])
    st_conv = din("st_conv", [ngr, 24, 4, 128])
    o_k = dout("o_k", [ngr, NTOK, 2, 128])
    o_v = dout("o_v", [ngr, NTOK, 2, 128])
    o_S = dout("o_S", [ngr, 9, 2, 128, 128])
    o_conv = dout("o_conv", [ngr, 27, 4, 128])
    o_oa = dout("o_oa", [ngr, 2, 128, NTOK], BF16)
    o_ob = dout("o_ob", [ngr, 2, 128, NTOK], BF16)
    XN = Tl(dt("XN", [NG, 128, NKC, GT], BF16), "XN", True)
    SQT = Tl(dt("SQT", [ngr, 2, 128, NTOK], BF16), "SQT", True)
    SKT = Tl(dt("SKT", [ngr, 2, 128, NTOK], BF16), "SKT", True)
    SV = Tl(dt("SV", [ngr, 2, NTOK, 128], BF16), "SV", True)

    def DR(ap, key):
        return V(ap, key)

    with contextlib.ExitStack() as es:
        P = Prog(nc, es)
        cnt = [0]

        def sb(shape, dtype=F32, name=None):
            cnt[0] += 1
            nm = name or ("t%d" % cnt[0])
            return Tl(es.enter_context(nc.sbuf_tensor("sb_" + nm, list(shape), dtype)), nm)

        def ps(shape, dtype=F32, name=None):
            cnt[0] += 1
            nm = name or ("p%d" % cnt[0])
            return Tl(es.enter_context(nc.psum_tensor("ps_" + nm, list(shape), dtype)), nm)

        USZ = 146 * 1024
        U = es.enter_context(nc.sbuf_tensor("sb_U", [128, USZ // 2], BF16))

        class Carver:
            def __init__(self, tag):
                self.off = 0
                self.tag = tag

            def __call__(self, shape, dtype=F32, name=None):
                esz = 4 if dtype == F32 else 2
                n = 1
                for d_ in shape[1:]:
                    n *= d_
                nb = (n * esz + 63) // 64 * 64
                assert self.off + nb <= USZ, (self.tag, name, self.off, nb, USZ)
                ap = U[:, self.off // 2:(self.off + n * esz) // 2]
                if dtype == F32:
                    ap = ap.bitcast(F32)
                if len(shape) == 3:
                    ap = ap.rearrange("p (a b) -> p a b", a=shape[1])
                ap = ap[0:shape[0]]
                self.off += nb
                return Tl(ap, "%s_%s" % (self.tag, name))

        cf = sb([128, 9, 128], F32, "cf")
        cb = sb([128, 2560], BF16, "cb")
        P.dma("sp", cf[:], DR(cf_in, "in_cf"))
        P.dma("sp", cb[:], DR(cb_in, "in_cb"))
        identF = cf[:, CF_ID, :]
        onesF = cf[:, CF_ONES, :]
        identB = cb[:, 0:128]
        NU = cb[:, 128:256]
        NO = cb[:, 256:384]
        MSb = cb[:, 384:512]

        def MG(r):
            return cb[:, 512 + 512 * r: 1024 + 512 * r]

        pbig = [ps([128, 512], F32, "pbig%d" % i) for i in range(4)]
        psm = ps([128, 16, 128], F32, "psm")
        pbig_rr = RR(pbig)
        psm_rr = RR(list(range(16)))

        def sm(cols=128, rows=128):
            n_ = psm_rr.next()
            bank = n_ % 4
            j = bank * 4 + (n_ // 4) % 4
            return psm.k(bank)[0:rows, j, 0:cols]

        def pbf(pb):
            return pb.t[:].bitcast(BF16)

        rowbuf = sb([1, 512], F32, "rowbuf")

        def bcast_row(dst_tl, dram_ap, n, key):
            for c0 in range(0, n, 512):
                w = min(512, n - c0)
                P.dma("sp", rowbuf[0:1, 0:w], DR(dram_ap[:, c0:c0 + w], key))
                pb = pbig_rr.next()
                P.mm(pb[:, 0:w], cf[0:1, CF_ONES, :], rowbuf[0:1, 0:w])
                P.copy("dve", dst_tl[:, c0:c0 + w], pb[:, 0:w])

        gnwb = sb([128, 128], F32, "gnwb")
        bcast_row(gnwb, gdn_norm_w, 128, "in_gn")
        xg = sb([128, NKC, GT], BF16, "xg")
        Sst = sb([128, 18, 128], F32, "Sst")
        abrow = sb([128, 4], F32, "abrow")
        cw = sb([128, 4, 4], F32, "cw")
        cwrow = sb([4, 4, 128], F32, "cwrow")
        csT = sb([32, 128], F32, "csT")
        ss = sb([128, 4], F32, "ss")

        C0 = Carver("s0")
        w1b = C0([128, D], F32, "w1b")
        xt = C0([128, D], F32, "xt")
        xs = C0([128, D], BF16, "xs")
        bcast_row(w1b, norm1_w, D, "in_n1")
        for t in range(NTP + 1):
            g, tsub = divmod(t, NTG)
            src = x_prompt[t * 128:(t + 1) * 128, :] if t < NTP else x_sample
            P.dma("sp", xt[:], DR(src, "in_x"))
            P.memset("pool", ss[:, 0:1], 0.0)
            P.act(xs[:], xt[:], AF.Square, accum=ss[:, 0:1])
            P.ts("dve", ss[:, 1:2], ss[:, 0:1], 1.0 / D, ALU.mult, EPS, ALU.add)
            P.rsqrt(ss[:, 1:2], ss[:, 1:2])
            P.stt("dve", xs[:], xt[:], ss[:, 1:2], w1b[:], ALU.mult, ALU.mult)
            for j in range(8):
                pb = pbig_rr.next()
                pbv = pbf(pb)
                for q in range(4):
                    kc = j * 4 + q
                    P.tr(V(pbv[:, q * 128:(q + 1) * 128], pb.name), xs[:, kc * 128:(kc + 1) * 128], identB)
                src_v = V(pbv[:, 0:512].rearrange("p (a b) -> p a b", a=4), pb.name)
                P.copy("act" if j % 2 == 0 else "dve", xg[:, j * 4:(j + 1) * 4, tsub * 128:(tsub + 1) * 128], src_v)
            if tsub == NTG - 1 and t < NTP:
                P.dma("sp", XN.k(g)[g], xg[:])
            elif t == NTP:
                P.dma("sp", XN.k(g)[g, :, :, 0:128], xg[:, :, 0:128])
        P.barrier()
        if P1LIM == 0:
            P.emit()
            return nc

        for gi, g in enumerate(groups):
            CA = Carver("A%d" % gi)
            W1 = CA([128, NKC, W1C], BF16, "W1")
            raw = [CA([128, GT + 3], F32, "raw%d" % i) for i in range(4)]
            raws = [CA([128, 8, 19], F32, "raws%d" % i) for i in range(4)]
            cst = CA([128, 24], F32, "cst")
            cacc = CA([128, GT], F32, "cacc")
            cvo = [CA([128, GT], F32, "cvo%d" % i) for i in range(4)]
            sqt = CA([128, GT], F32, "sqt")
            rn = CA([128, GT], F32, "rn")
            qn = CA([128, GT], F32, "qn")
            kn = CA([128, GT], F32, "kn")
            ztok = CA([128, NTG, 260], F32, "ztok")
            kvtok = CA([128, 512], F32, "kvtok")
            vbf = CA([128, 256], BF16, "vbf")
            fbf = [CA([128, GT], BF16, "fbf%d" % i) for i in range(2)]
            gd = {}
            for nm in ["ktok", "kkS", "qkS", "Ah", "Db", "dec", "decT", "erow", "t1", "t2", "t3", "aqkT", "X0", "X1",
                       "Mc0", "Mc1", "MT0", "MT1", "IpM", "vb", "kbg", "kd", "qgT", "u0S", "wTS", "u", "on", "sz"]:
                gd[nm] = CA([128, 128], F32, "gd_" + nm)
            gsm = CA([128, 64], F32, "gsm")
            ogb = CA([128, 128], BF16, "ogb")
            oaT = CA([128, 128], BF16, "oaT")
            wTm = CA([128, 8, 128], F32, "wTm")
            qgTm = CA([128, 8, 128], F32, "qgTm")
            kdm = CA([128, 128], F32, "kdm")
            grm = CA([128, 8, 2], F32, "grm")
            eglb = CA([128, 16], F32, "eglb")
            frr = RR([0, 1])

            for c0 in range(0, W1C, 512):
                w = min(512, W1C - c0)
                for k0 in range(0, NKC, 8):
                    src = w1[gi].rearrange("(kc p) n -> p kc n", p=128)[:, k0:k0 + 8, c0:c0 + w]
                    P.dma("poolq", W1[:, k0:k0 + 8, c0:c0 + w], DR(src, "in_w1"))
            P.dma("sp", cwrow[:], DR(convw[gi], "in_cw"))
            for c in range(4):
                s_ = sm(4)
                P.mm(s_, cwrow[0:4, c, :], cf[0:4, CF_ID, 0:4])
                P.copy("dve", cw[:, c, :], s_)
            P.dma("sp", rowbuf[0:1, 0:2], DR(dtb[gi], "in_dtb"))
            P.dma("sp", rowbuf[0:1, 2:4], DR(alog[gi], "in_alog"))
            P.act(rowbuf[0:1, 2:4], rowbuf[0:1, 2:4], AF.Exp)
            P.ts("dve", rowbuf[0:1, 2:4], rowbuf[0:1, 2:4], -1.0, ALU.mult)
            s_ = sm(4)
            P.mm(s_, cf[0:1, CF_ONES, :], rowbuf[0:1, 0:4])
            P.copy("dve", abrow[:], s_)
            P.memset("pool", Sst[:, 0:2, :], 0.0)
            for b in range(8):
                for h in range(2):
                    P.dma("sp", Sst.k(2 + 2 * b + h)[:, 2 + 2 * b + h, :], DR(st_S[gi, b, h], "in_S"))
            for c in range(4):
                P.memset("pool", raw[c][:, 0:3], 0.0)

            for G in range(NG if P1SUB > 'a' else 0):
                is_s = (G == NGP)
                nt = 1 if is_s else NTG
                ntk = 128 if is_s else GT
                tok0 = G * GT
                if is_s:
                    P.dma("sp", xg[:, :, 0:128], XN.k(G)[G, :, :, 0:128])
                else:
                    P.dma("sp", xg[:], XN.k(G)[G])
                for c in range(8):
                    pb = pbig_rr.next()
                    for kc in range(NKC):
                        P.mm(pb[:, 0:ntk], W1[:, kc, c * 128:(c + 1) * 128], xg[:, kc, 0:ntk],
                             start=(kc == 0), stop=(kc == NKC - 1))
                    if c < 4:
                        if is_s:
                            P.copy("act", raws[c][:, :, 3:19], V(pb.t[:, 0:128].rearrange("p (b t) -> p b t", b=8), pb.name))
                        else:
                            P.copy("act", raw[c][:, 3:GT + 3], pb[:, 0:GT])
                    else:
                        fb = fbf[frr.next()]
                        h = (c - 4) % 2
                        if c < 6:
                            P.ts("dve", fb[:, 0:ntk], pb[:, 0:ntk], float(HD ** -0.5), ALU.mult)
                            P.dma("sp", SQT.k((gi, h))[gi, h, :, tok0:tok0 + ntk], fb[:, 0:ntk])
                        else:
                            P.copy("dve", fb[:, 0:ntk], pb[:, 0:ntk])
                            P.dma("sp", SKT.k((gi, h))[gi, h, :, tok0:tok0 + ntk], fb[:, 0:ntk])
                for ti in range(nt if P1SUB > 'b' else 0):
                    pb = pbig_rr.next()
                    for kc in range(NKC):
                        P.mm(pb[:, 0:512], xg[:, kc, ti * 128:(ti + 1) * 128], W1[:, kc, 768:1280],
                             start=(kc == 0), stop=(kc == NKC - 1))
                    P.copy("act", kvtok[:], pb[:, 0:512])
                    P.copy("dve", vbf[:], kvtok[:, 256:512])
                    t0 = tok0 + ti * 128
                    if P1SUB > 'c':
                        P.dma("sp", DR(o_k[gi, t0:t0 + 128], "o_k"), V(kvtok.t[:, 0:256].rearrange("p (h d) -> p h d", h=2), kvtok.name))
                        P.dma("sp", DR(o_v[gi, t0:t0 + 128], "o_v"), V(kvtok.t[:, 256:512].rearrange("p (h d) -> p h d", h=2), kvtok.name))
                    if P1SUB > 'd':
                        for h in range(2):
                            P.dma("sp", SV.k((gi, h))[gi, h, t0:t0 + 128, :], vbf[:, h * 128:(h + 1) * 128])
                    if P1SUB > 'e':
                        pb = pbig_rr.next()
                        for kc in range(NKC):
                            P.mm(pb[:, 0:260], xg[:, kc, ti * 128:(ti + 1) * 128], W1[:, kc, 1280:1540],
                                 start=(kc == 0), stop=(kc == NKC - 1))
                        P.copy("act", ztok[:, ti, :], pb[:, 0:260])
                if P1LIM == 1:
                    continue
                for c in range(4):
                    if is_s:
                        P.dma("sp", csT[0:24, :], DR(st_conv[gi, :, c, :], "in_conv"))
                        s_ = sm(24)
                        P.mm(s_, csT[0:24, :], cf[0:24, CF_ID, 0:24])
                        P.copy("dve", raws[c][:, :, 0:3], V(s_.ap.rearrange("p (b i) -> p b i", b=8), s_.key))
                        xin = [raws[c][:, :, i:i + 16] for i in range(4)]
                        acc = V(cacc.t[:, 0:128].rearrange("p (b t) -> p b t", b=8), cacc.name)
                        outv = V(cvo[c].t[:, 0:128].rearrange("p (b t) -> p b t", b=8), cvo[c].name)
                    else:
                        xin = [raw[c][:, i:i + GT] for i in range(4)]
                        acc = cacc[:, 0:GT]
                        outv = cvo[c][:, 0:GT]
                    P.ts("dve", acc, xin[0], cw[:, c, 0:1], ALU.mult)
                    for i in range(1, 4):
                        P.stt("dve", acc, xin[i], cw[:, c, i:i + 1], acc, ALU.mult, ALU.add)
                    P.act(outv, acc, AF.Silu)
                    if is_s:
                        P.copy("dve", V(cst.t[:, 0:24].rearrange("p (b i) -> p b i", b=8), cst.name), raws[c][:, :, 16:19])
                        s_ = sm(128, 24)
                        P.mm(s_, cst[:, 0:24], identF)
                        P.copy("dve", csT[0:24, :], s_)
                        P.dma("sp", DR(o_conv[gi, 3:27, c, :], "o_conv"), csT[0:24, :])
                    else:
                        P.copy("dve", raw[c][:, 0:3], raw[c][:, GT:GT + 3])
                        if G == NGP - 1:
                            s_ = sm(128, 3)
                            P.mm(s_, raw[c][:, 0:3], identF)
                            P.copy("dve", csT[0:3, :], s_)
                            P.dma("sp", DR(o_conv[gi, 0:3, c, :], "o_conv"), csT[0:3, :])
                for (src_t, dst_t, scl) in ((cvo[0], qn, float(HD ** -0.5)), (cvo[1], kn, 1.0)):
                    P.tt("dve", sqt[:, 0:ntk], src_t[:, 0:ntk], src_t[:, 0:ntk], ALU.mult)
                    pb = pbig_rr.next()
                    P.mm(pb[:, 0:ntk], onesF, sqt[:, 0:ntk])
                    P.ts("dve", rn[:, 0:ntk], pb[:, 0:ntk], EPS, ALU.add)
                    P.rsqrt(rn[:, 0:ntk], rn[:, 0:ntk])
                    P.stt("dve", dst_t[:, 0:ntk], src_t[:, 0:ntk], scl, rn[:, 0:ntk], ALU.mult, ALU.mult)
                if P1LIM == 2:
                    continue
                mUT = cf[:, CF_UTB if is_s else CF_UT, :]
                mSL = cf[:, CF_SLB if is_s else CF_SL, :]
                mSLT = cf[:, CF_SLTB if is_s else CF_SLT, :]
                for ti in range(nt):
                    cs_ = slice(ti * 128, (ti + 1) * 128)
                    t0 = tok0 + ti * 128
                    qT = qn[:, cs_]
                    kT = kn[:, cs_]
                    s1 = sm()
                    P.mm(s1, kT, identF)
                    P.copy("act", gd["ktok"][:], s1)
                    s2 = sm()
                    P.mm(s2, kT, kT)
                    P.copy("dve", gd["kkS"][:], s2)
                    s3 = sm()
                    P.mm(s3, kT, qT)
                    P.copy("act", gd["qkS"][:], s3)
                    a_ = ztok[:, ti, 256:258]
                    b_ = ztok[:, ti, 258:260]
                    P.tt("dve", gsm[:, 0:2], a_, abrow[:, 0:2], ALU.add)
                    P.act(gsm[:, 2:4], gsm[:, 0:2], AF.Exp)
                    P.act(gsm[:, 4:6], gsm[:, 2:4], AF.Ln, bias=1.0)
                    P.tt("dve", gsm[:, 6:8], gsm[:, 4:6], abrow[:, 2:4], ALU.mult)
                    P.act(gsm[:, 8:10], b_, AF.Exp, scale=-1.0)
                    P.ts("dve", gsm[:, 8:10], gsm[:, 8:10], 1.0, ALU.add)
                    P.recip(gsm[:, 10:12], gsm[:, 8:10])
                    gcol = gsm[:, 6:8]
                    s4 = sm(2)
                    P.mm(s4, mUT, gcol)
                    P.act(gsm[:, 12:14], s4, AF.Exp)
                    s5 = sm(2)
                    P.mm(s5, mSL, gcol)
                    P.act(gsm[:, 14:16], s5, AF.Exp)
                    P.tt("dve", gsm[:, 16:18], gsm[:, 10:12], gsm[:, 12:14], ALU.mult)
                    if is_s:
                        for b in range(8):
                            P.ts("dve", grm[:, b, :], gcol, cf[:, CF_ROWM, b:b + 1], ALU.mult)
                        s6 = sm(16)
                        P.mm(s6, onesF, V(grm.t[:].rearrange("p b h -> p (b h)"), grm.name))
                        P.act(eglb[:, 0:16], s6, AF.Exp)
                    else:
                        s6 = sm(2)
                        P.mm(s6, onesF, gcol)
                        P.act(gsm[:, 18:20], s6, AF.Exp)
                    for h in range(2):
                        beta = gsm[:, 10 + h:11 + h]
                        P.ts("dve", gd["Ah"][:], mUT, gsm[:, 6 + h:7 + h], ALU.mult)
                        P.ts("dve", gd["Db"][:], identF, beta, ALU.mult)
                        d1 = sm()
                        P.mm(d1, gd["Ah"][:], mSL)
                        P.act(gd["dec"][:], d1, AF.Exp)
                        d2 = sm()
                        P.mm(d2, mSL, gd["Ah"][:])
                        P.act(gd["decT"][:], d2, AF.Exp)
                        d3 = sm()
                        P.mm(d3, onesF, gd["Ah"][:])
                        P.act(gd["erow"][:], d3, AF.Exp)
                        d4 = sm()
                        P.mm(d4, onesF, gd["Db"][:])
                        P.tt("dve", gd["t1"][:], gd["kkS"][:], gd["dec"][:], ALU.mult)
                        P.stt("dve", gd["MT0"][:], gd["t1"][:], beta, mSL, ALU.mult, ALU.mult)
                        P.tt("dve", gd["t2"][:], gd["kkS"][:], gd["decT"][:], ALU.mult)
                        P.tt("dve", gd["t2"][:], gd["t2"][:], mSLT, ALU.mult)
                        P.tt("dve", gd["Mc0"][:], gd["t2"][:], d4, ALU.mult)
                        P.tt("dve", gd["t3"][:], gd["qkS"][:], gd["decT"][:], ALU.mult)
                        P.tt("dve", gd["aqkT"][:], gd["t3"][:], mUT, ALU.mult)
                        P.tt("dve", gd["X0"][:], identF, gd["Mc0"][:], ALU.subtract)
                        cur = 0
                        nsteps = 3 if is_s else 6
                        for st_ in range(nsteps):
                            Mc, MT, Xc = gd["Mc%d" % cur], gd["MT%d" % cur], gd["X%d" % cur]
                            Mn, MTn, Xn = gd["Mc%d" % (1 - cur)], gd["MT%d" % (1 - cur)], gd["X%d" % (1 - cur)]
                            pa = sm()
                            P.mm(pa, MT[:], Mc[:])
                            pbb = sm()
                            P.mm(pbb, Mc[:], MT[:])
                            if st_ < nsteps - 1:
                                P.copy("act", Mn[:], pa)
                                P.copy("act", MTn[:], pbb)
                            P.tt("dve", gd["IpM"][:], pbb, identF, ALU.add)
                            pc = sm()
                            P.mm(pc, gd["IpM"][:], Xc[:])
                            P.copy("dve", Xn[:], pc)
                            cur = 1 - cur
                        Xf = gd["X%d" % cur]
                        vT = cvo[2 + h][:, cs_]
                        s7 = sm()
                        P.mm(s7, vT, identF)
                        P.ts("dve", gd["vb"][:], s7, beta, ALU.mult)
                        P.ts("dve", gd["kbg"][:], gd["ktok"][:], gsm[:, 16 + h:17 + h], ALU.mult)
                        P.ts("dve", gd["kd"][:], gd["ktok"][:], gsm[:, 14 + h:15 + h], ALU.mult)
                        P.tt("dve", gd["qgT"][:], qT, gd["erow"][:], ALU.mult)
                        s8 = sm()
                        P.mm(s8, Xf[:], gd["vb"][:])
                        P.copy("act", gd["u0S"][:], s8)
                        s9 = sm()
                        P.mm(s9, gd["kbg"][:], Xf[:])
                        P.copy("act", gd["wTS"][:], s9)
                        ws = sm()
                        op_ = sm()
                        if not is_s:
                            Sv = Sst.k(h)[:, h, :]
                            P.mm(ws, gd["wTS"][:], Sv)
                            P.tt("dve", gd["u"][:], gd["u0S"][:], ws, ALU.subtract)
                            P.mm(op_, gd["qgT"][:], Sv, start=True, stop=False)
                            P.mm(op_, gd["aqkT"][:], gd["u"][:], start=False, stop=True)
                        else:
                            P.memset("pool", wTm[:], 0.0)
                            P.memset("pool", qgTm[:], 0.0)
                            for b in range(8):
                                P.copy("pool", wTm[:, b, 16 * b:16 * b + 16], gd["wTS"][:, 16 * b:16 * b + 16])
                                P.copy("pool", qgTm[:, b, 16 * b:16 * b + 16], gd["qgT"][:, 16 * b:16 * b + 16])
                            for b in range(8):
                                Sv = Sst.k(2 + 2 * b + h)[:, 2 + 2 * b + h, :]
                                P.mm(ws, wTm[:, b, :], Sv, start=(b == 0), stop=(b == 7))
                            P.tt("dve", gd["u"][:], gd["u0S"][:], ws, ALU.subtract)
                            for b in range(8):
                                Sv = Sst.k(2 + 2 * b + h)[:, 2 + 2 * b + h, :]
                                P.mm(op_, qgTm[:, b, :], Sv, start=(b == 0), stop=False)
                            P.mm(op_, gd["aqkT"][:], gd["u"][:], start=False, stop=True)
                        P.memset("pool", gsm[:, 20:21], 0.0)
                        P.act(gd["on"][:], op_, AF.Square, accum=gsm[:, 20:21])
                        P.ts("dve", gsm[:, 21:22], gsm[:, 20:21], 1.0 / HD, ALU.mult, EPS, ALU.add)
                        P.rsqrt(gsm[:, 21:22], gsm[:, 21:22])
                        P.stt("dve", gd["on"][:], op_, gsm[:, 21:22], gnwb[:], ALU.mult, ALU.mult)
                        P.act(gd["sz"][:], ztok[:, ti, h * 128:(h + 1) * 128], AF.Silu)
                        P.tt("dve", ogb[:], gd["on"][:], gd["sz"][:], ALU.mult)
                        pbt = pbig_rr.next()
                        pbtv = V(pbf(pbt)[:, 0:128], pbt.name)
                        P.tr(pbtv, ogb[:], identB)
                        P.copy("act", oaT[:], pbtv)
                        P.dma("sp", DR(o_oa[gi, h, :, t0:t0 + 128], "o_oa"), oaT[:])
                        if not is_s:
                            dS = sm()
                            P.mm(dS, gd["kd"][:], gd["u"][:])
                            Sv = Sst.k(h)[:, h, :]
                            P.stt("dve", Sv, Sv, gsm[:, 18 + h:19 + h], dS, ALU.mult, ALU.add)
                            if G == NGP - 1 and ti == nt - 1:
                                P.dma("sp", DR(o_S[gi, 0, h], "o_S"), Sv)
                        else:
                            for b in range(8):
                                P.ts("dve", kdm[:], gd["kd"][:], cf[:, CF_ROWM, b:b + 1], ALU.mult)
                                dS = sm()
                                P.mm(dS, kdm[:], gd["u"][:])
                                Sv = Sst.k(2 + 2 * b + h)[:, 2 + 2 * b + h, :]
                                P.stt("dve", Sv, Sv, eglb[:, 2 * b + h:2 * b + h + 1], dS, ALU.mult, ALU.add)
                                P.dma("sp", DR(o_S[gi, 1 + b, h], "o_S"), Sv)
            P.barrier()
            if P1LIM <= 3:
                continue

            CB = Carver("B%d" % gi)
            KT = CB([128, SEQ], BF16, "KT")
            VV = CB([128, NTP, 128], BF16, "VV")
            QT = CB([128, SEQ], BF16, "QT")
            ae = [CB([128, 512], F32, "ae%d" % i) for i in range(2)]
            asp = [CB([128, 512], BF16, "asp%d" % i) for i in range(2)]
            aX = [CB([128, 512], F32, "aX%d" % i) for i in range(2)]
            aat = [CB([128, 512], BF16, "aat%d" % i) for i in range(2)]
            Lsb = CB([128, 512], F32, "Lsb")
            obf = CB([128, 512], BF16, "obf")
            ckb = [CB([128, 128], BF16, "ckb%d" % i) for i in range(2)]
            ckT = [CB([128, 128], BF16, "ckT%d" % i) for i in range(2)]
            cvb = [CB([128, 128], BF16, "cvb%d" % i) for i in range(2)]
            qs = CB([128, 16], BF16, "qs")
            ksn = CB([128, 16], BF16, "ksn")
            vsn = CB([16, 128], BF16, "vsn")
            rrs = {k: RR([0, 1]) for k in ("ae", "asp", "aX", "aat", "ck", "A")}
            O_ps, A_ps, T_ps = pbig[0], [pbig[1], pbig[2]], pbig[3]

            def attn(QTv, nq, blocks, out_dst):
                P.memset("pool", Lsb[:, 0:nq], 0.0)
                O = O_ps
                nb = len(blocks)
                for bi, b in enumerate(blocks):
                    ns = b["ns"]
                    if "prep" in b:
                        b["kT"], b["v"] = b["prep"]()
                    A = A_ps[rrs["A"].next()]
                    T = T_ps
                    e = ae[rrs["ae"].next()]
                    sp_ = asp[rrs["asp"].next()]
                    X = aX[rrs["aX"].next()]
                    at = aat[rrs["aat"].next()]
                    P.mm(A[0:ns, 0:nq], b["kT"], QTv, start=True, stop=True)
                    P.act(e[0:ns, 0:nq], A[0:ns, 0:nq], AF.Exp)
                    P.act(sp_[0:ns, 0:nq], e[0:ns, 0:nq], AF.Ln, bias=1.0)
                    if b["mask"] is not None:
                        P.tt("dve", sp_[0:ns, 0:nq], sp_[0:ns, 0:nq], b["mask"], ALU.mult)
                    P.mm(A[0:ns, 0:nq], b["kT"], QTv, start=True, stop=False)
                    P.mm(A[0:ns, 0:nq], V(NU.ap[0:ns, 0:ns], NU.key), sp_[0:ns, 0:nq], start=False, stop=True)
                    if bi < nb - 1:
                        P.mm(T[:, 0:nq], V(NO.ap[0:ns, :], NO.key), sp_[0:ns, 0:nq])
                    P.tt("dve", X[0:ns, 0:nq], A[0:ns, 0:nq], Lsb[0:ns, 0:nq], ALU.add)
                    P.act(at[0:ns, 0:nq], X[0:ns, 0:nq], AF.Exp)
                    if b["mask"] is not None:
                        P.tt("dve", at[0:ns, 0:nq], at[0:ns, 0:nq], b["mask"], ALU.mult)
                    P.mm(O[:, 0:nq], b["v"], at[0:ns, 0:nq], start=(bi == 0), stop=(bi == nb - 1))
                    if bi < nb - 1:
                        P.tt("dve", Lsb[:, 0:nq], Lsb[:, 0:nq], T[:, 0:nq], ALU.add)
                out_dst(O)

            for h in range(2):
                P.dma("sp", KT[:], SKT.k((gi, h))[gi, h, :, 0:SEQ])
                P.dma("sp", QT[:], SQT.k((gi, h))[gi, h, :, 0:SEQ])
                for n0 in range(0, NTP, 8):
                    n1 = min(NTP, n0 + 8)
                    P.dma("sp", VV[:, n0:n1, :],
                          V(SV.t[gi, h, n0 * 128:n1 * 128, :].rearrange("(n p) d -> p n d", p=128), ("SV", (gi, h))))
                for c0 in range(0, SEQ, 4096):
                    pass
                for Gq in range(NQG):
                    blocks = []
                    for n in range(4 * Gq + 3, -1, -1):
                        r = n - 4 * Gq
                        blocks.append(dict(kT=KT[:, n * 128:(n + 1) * 128], v=VV[:, n, :], ns=128,
                                           mask=(MG(r) if r >= 0 else None)))

                    def outp(O, Gq=Gq, h=h):
                        P.copy("act", obf[:], O[:, 0:512])
                        P.dma("sp", DR(o_ob[gi, h, :, Gq * 512:(Gq + 1) * 512], "o_ob"), obf[:])
                    attn(QT[:, Gq * 512:(Gq + 1) * 512], 512, blocks, outp)
                for b in range(8 if P1LIM > 4 else 0):
                    ts0 = SEQ + 16 * b
                    P.dma("sp", qs[:], SQT.k((gi, h))[gi, h, :, ts0:ts0 + 16])
                    P.dma("sp", ksn[:], SKT.k((gi, h))[gi, h, :, ts0:ts0 + 16])
                    P.dma("sp", vsn[:], SV.k((gi, h))[gi, h, ts0:ts0 + 16, :])
                    blocks = [dict(kT=ksn[:], v=vsn[:], ns=16, mask=V(MSb.ap[0:16, 0:16], MSb.key))]
                    for n in range(PAST // 128 - 1, -1, -1):
                        def prep(n=n, b=b, h=h):
                            i_ = rrs["ck"].next()
                            P.dma("poolq", ckb[i_][:], DR(cache_k[gi, b, n * 128:(n + 1) * 128, h, :], "in_ck"))
                            P.dma("poolq", cvb[i_][:], DR(cache_v[gi, b, n * 128:(n + 1) * 128, h, :], "in_cv"))
                            pbt = A_ps[rrs["A"].next()]
                            pbtv = V(pbf(pbt)[:, 0:128], pbt.name)
                            P.tr(pbtv, ckb[i_][:], identB)
                            P.copy("dve", ckT[i_][:], pbtv)
                            return ckT[i_][:], cvb[i_][:]
                        blocks.append(dict(prep=prep, ns=128, mask=None))

                    def outs(O, ts0=ts0, h=h):
                        P.copy("act", obf[:, 0:16], O[:, 0:16])
                        P.dma("sp", DR(o_ob[gi, h, :, ts0:ts0 + 16], "o_ob"), obf[:, 0:16])
                    attn(qs[:], 16, blocks, outs)
            P.barrier()
        P.emit()
    return nc


def build_phase2(T8, PG):
    nc = bass.Bass("TRN2", target_bir_lowering=False)
    dt = nc.dram_tensor
    NT = T8 + 16

    def din(name, shape, dtype=F32):
        return dt(name, list(shape), dtype, kind="ExternalInput").ap()

    x_own = din("x_own", [NT, D])
    oaT_in = din("oaT", [16, 128, NT], BF16)
    obT_in = din("obT", [16, 128, NT], BF16)
    cf_in = din("cf", [128, 9, 128])
    cb_in = din("cb", [128, 2560], BF16)
    nw_in = din("nw3", [96, 128])
    w_gate = din("w_gate", [D, 2 * D])
    w_gdn_o = din("w_gdn_o", [2048, D])
    w_sb_o = din("w_sb_o", [2048, D])
    w_out = din("w_out", [D, D])
    w_up = din("w_up", [D, DFF])
    w_down = din("w_down", [DFF, D])
    y_out = dt("y", [NT, D], F32, kind="ExternalOutput").ap()

    def DR(ap, key):
        return V(ap, key)

    with contextlib.ExitStack() as es:
        P = Prog(nc, es)

        def sb(shape, dtype=F32, name=None):
            return Tl(es.enter_context(nc.sbuf_tensor("sb_" + name, list(shape), dtype)), name)

        def ps(shape, dtype=F32, name=None):
            return Tl(es.enter_context(nc.psum_tensor("ps_" + name, list(shape), dtype)), name)

        cf = sb([128, 9, 128], F32, "cf")
        cb = sb([128, 2560], BF16, "cb")
        P.dma("sp", cf[:], DR(cf_in, "in_cf"))
        P.dma("sp", cb[:], DR(cb_in, "in_cb"))
        identF = cf[:, CF_ID, :]
        onesF = cf[:, CF_ONES, :]
        identB = cb[:, 0:128]
        pbig = [ps([128, 512], F32, "pbig%d" % i) for i in range(8)]
        pbig_rr = RR(pbig)

        def pbf(pb):
            return pb.t[:].bitcast(BF16)

        nwr = sb([96, 128], F32, "nwr")
        nwT = sb([128, 96], F32, "nwT")
        P.dma("sp", nwr[:], DR(nw_in, "in_nw"))
        pb = pbig_rr.next()
        P.mm(pb[:, 0:96], nwr[:], cf[0:96, CF_ID, 0:96])
        P.copy("dve", nwT[:], pb[:, 0:96])

        xt = sb([128, D], F32, "xt")
        xs = sb([128, D], BF16, "xs")
        ss = sb([128, 4], F32, "ss")
        xg = sb([128, NKC, PG], BF16, "xg")
        hT = sb([128, NKC, PG], F32, "hT")
        oaT = sb([128, 16, PG], BF16, "oaT")
        obT = sb([128, 16, PG], BF16, "obT")
        mT = sb([128, NKC, PG], BF16, "mT")
        aT = sb([128, 32, PG], BF16, "aT")
        wbuf = [sb([128, NKC, 128], BF16, "wb%d" % i) for i in range(6)]
        wrr = RR(list(range(6)))
        g0 = sb([128, PG], F32, "g0")
        g1 = sb([128, PG], F32, "g1")
        t0_ = sb([128, PG], F32, "t0_")
        t1_ = sb([128, PG], F32, "t1_")
        sq = sb([128, PG], F32, "sq")
        rnb = sb([128, PG], F32, "rnb")

        def load_w(src2d, k0, nk, c0):
            wb = wbuf[wrr.next()]
            for kk0 in range(0, nk, 8):
                kn_ = min(8, nk - kk0)
                src = src2d.rearrange("(kc p) n -> p kc n", p=128)[:, k0 + kk0:k0 + kk0 + kn_, c0:c0 + 128]
                P.dma("poolq", wb[:, kk0:kk0 + kn_, :], DR(src, "in_w"))
            return wb

        def rms_rows(N, wofs, dst):
            pss = pbig_rr.next()
            for c in range(32):
                P.tt("dve", sq[:, 0:N], hT[:, c, 0:N], hT[:, c, 0:N], ALU.mult)
                P.mm(pss[:, 0:N], onesF, sq[:, 0:N], start=(c == 0), stop=(c == 31))
            P.ts("dve", rnb[:, 0:N], pss[:, 0:N], 1.0 / D, ALU.mult, EPS, ALU.add)
            P.rsqrt(rnb[:, 0:N], rnb[:, 0:N])
            for c in range(32):
                P.stt("dve", dst[:, c, 0:N], hT[:, c, 0:N], nwT[:, wofs + c:wofs + c + 1], rnb[:, 0:N], ALU.mult, ALU.mult)

        groups = [(i * PG, PG) for i in range(T8 // PG)] + [(T8, 16)]
        for (tk0, N) in groups:
            for t0 in range(0, N, 128):
                n = min(128, N - t0)
                P.dma("sp", xt[0:n, :], DR(x_own[tk0 + t0:tk0 + t0 + n, :], "in_x"))
                P.memset("pool", ss[:, 0:1], 0.0)
                P.act(xs[0:n, :], xt[0:n, :], AF.Square, accum=ss[0:n, 0:1])
                P.ts("dve", ss[0:n, 1:2], ss[0:n, 0:1], 1.0 / D, ALU.mult, EPS, ALU.add)
                P.rsqrt(ss[0:n, 1:2], ss[0:n, 1:2])
                P.ts("dve", xs[0:n, :], xt[0:n, :], ss[0:n, 1:2], ALU.mult)
                for j in range(8):
                    pb = pbig_rr.next()
                    pbv = pbf(pb)
                    for q in range(4):
                        kc = j * 4 + q
                        P.tr(V(pbv[:, q * 128:q * 128 + n], pb.name), xs[0:n, kc * 128:(kc + 1) * 128],
                             V(identB.ap[0:n, 0:n], identB.key))
                    for q in range(4):
                        kc = j * 4 + q
                        P.ts("dve" if q % 2 else "act_ts", xg[:, kc, t0:t0 + n], V(pbv[:, q * 128:q * 128 + n], pb.name),
                             nwT[:, kc:kc + 1], ALU.mult)
                for j in range(8):
                    pb = pbig_rr.next()
                    for q in range(4):
                        kc = j * 4 + q
                        P.mm(pb[:, q * 128:q * 128 + n], xt[0:n, kc * 128:(kc + 1) * 128], cf[0:n, CF_ID, 0:n])
                    src_v = V(pb.t[:, 0:512].rearrange("p (a b) -> p a b", a=4)[:, :, 0:n], pb.name)
                    P.copy("dve" if j % 2 == 0 else "act", hT[:, j * 4:(j + 1) * 4, t0:t0 + n], src_v)
            P.dma("sp", oaT[:, :, 0:N], DR(oaT_in.rearrange("h p t -> p h t")[:, :, tk0:tk0 + N], "in_oa"))
            P.dma("sp", obT[:, :, 0:N], DR(obT_in.rearrange("h p t -> p h t")[:, :, tk0:tk0 + N], "in_ob"))
            for c in range(32):
                wa = load_w(w_gate, 0, 32, c * 128)
                wbb = load_w(w_gate, 0, 32, D + c * 128)
                wga = load_w(w_gdn_o, 0, 16, c * 128)
                wso = load_w(w_sb_o, 0, 16, c * 128)
                pa, pb_, pya, pyb = (pbig_rr.next() for _ in range(4))
                for kc in range(32):
                    P.mm(pa[:, 0:N], wa[:, kc, :], xg[:, kc, 0:N], start=(kc == 0), stop=(kc == 31))
                for kc in range(32):
                    P.mm(pb_[:, 0:N], wbb[:, kc, :], xg[:, kc, 0:N], start=(kc == 0), stop=(kc == 31))
                for kc in range(16):
                    P.mm(pya[:, 0:N], wga[:, kc, :], oaT[:, kc, 0:N], start=(kc == 0), stop=(kc == 15))
                for kc in range(16):
                    P.mm(pyb[:, 0:N], wso[:, kc, :], obT[:, kc, 0:N], start=(kc == 0), stop=(kc == 15))
                P.act(g0[:, 0:N], pa[:, 0:N], AF.Sigmoid)
                P.act(g1[:, 0:N], pb_[:, 0:N], AF.Sigmoid)
                P.tt("dve", t0_[:, 0:N], g0[:, 0:N], pya[:, 0:N], ALU.mult)
                P.tt("dve", t1_[:, 0:N], g1[:, 0:N], pyb[:, 0:N], ALU.mult)
                P.tt("dve", mT[:, c, 0:N], t0_[:, 0:N], t1_[:, 0:N], ALU.add)
            for c in range(32):
                wo = load_w(w_out, 0, 32, c * 128)
                pa = pbig_rr.next()
                for kc in range(32):
                    P.mm(pa[:, 0:N], wo[:, kc, :], mT[:, kc, 0:N], start=(kc == 0), stop=(kc == 31))
                P.tt("dve", hT[:, c, 0:N], hT[:, c, 0:N], pa[:, 0:N], ALU.add)
            rms_rows(N, 32, mT)
            for qf in range(4):
                for f in range(32):
                    wu = load_w(w_up, 0, 32, (qf * 32 + f) * 128)
                    pa = pbig_rr.next()
                    for kc in range(32):
                        P.mm(pa[:, 0:N], wu[:, kc, :], mT[:, kc, 0:N], start=(kc == 0), stop=(kc == 31))
                    P.act(t0_[:, 0:N], pa[:, 0:N], AF.Relu)
                    P.tt("dve", aT[:, f, 0:N], t0_[:, 0:N], t0_[:, 0:N], ALU.mult)
                for c in range(32):
                    wd = load_w(w_down, qf * 32, 32, c * 128)
                    pa = pbig_rr.next()
                    for kc in range(32):
                        P.mm(pa[:, 0:N], wd[:, kc, :], aT[:, kc, 0:N], start=(kc == 0), stop=(kc == 31))
                    P.tt("dve", hT[:, c, 0:N], hT[:, c, 0:N], pa[:, 0:N], ALU.add)
            rms_rows(N, 64, hT)
            for t0 in range(0, N, 128):
                n = min(128, N - t0)
                for j in range(8):
                    pb = pbig_rr.next()
                    for q in range(4):
                        kc = j * 4 + q
                        P.mm(pb[0:n, q * 128:(q + 1) * 128], hT[:, kc, t0:t0 + n], identF)
                    P.copy("act" if j % 2 == 0 else "dve", xt[0:n, j * 512:(j + 1) * 512], pb[0:n, 0:512])
                P.dma("sp", DR(y_out[tk0 + t0:tk0 + t0 + n, :], "o_y"), xt[0:n, :])
        P.emit()
    return nc


def _w1_cols(g):
    def rng(a, n):
        return list(range(a, a + n))
    cols = []
    cols += rng(g * 128, 128)
    cols += rng(1024 + g * 128, 128)
    cols += rng(2048 + 2 * g * 128, 256)
    cols += rng(OFF_SB + 2 * g * 128, 256)
    cols += rng(OFF_SB + 2048 + 2 * g * 128, 256)
    cols += rng(OFF_SB + 4096 + 2 * g * 128, 256)
    cols += rng(OFF_Z + 2 * g * 128, 256)
    cols += rng(OFF_A + 2 * g, 2)
    cols += rng(OFF_B + 2 * g, 2)
    assert len(cols) == W1C
    return np.array(cols)


_CACHE = {}


def kernel(x_prompt, x_sample, cache_sb_k, cache_sb_v, state_gdn_S, state_gdn_conv,
           norm1_w, w_in, conv_w, A_log, dt_bias, gdn_norm_w, w_gdn_o, w_sb_o, w_out,
           norm2_w, w_up, w_down, final_norm_w):
    f = np.float32
    SEQ = x_prompt.shape[1]
    NTOK = SEQ + 128
    T8 = SEQ // 8
    PG = min(256, T8)
    cfc, cbc = host_consts()
    xp = np.ascontiguousarray(np.asarray(x_prompt, f)[0])
    xs_ = np.ascontiguousarray(np.asarray(x_sample, f).reshape(128, D))
    w_in0 = np.asarray(w_in, f)[0]
    convw = np.asarray(conv_w, f)[0]
    key1 = ("p1", SEQ)
    if key1 not in _CACHE:
        _CACHE[key1] = build_phase1(SEQ, [None], fused=False)
    nc1 = _CACHE[key1]
    in_maps = []
    n1 = np.ascontiguousarray(np.asarray(norm1_w, f).reshape(1, D))
    gnw = np.ascontiguousarray(np.asarray(gdn_norm_w, f).reshape(1, 128))
    for g in range(8):
        chs = [g * 128, 1024 + g * 128, 2048 + 2 * g * 128, 2048 + (2 * g + 1) * 128]
        cwg = np.stack([convw[:, c:c + 128] for c in chs], axis=1)
        stc = np.asarray(state_gdn_conv, f)[0].reshape(24, CONV_CH)
        stcg = np.stack([stc[:, c:c + 128] for c in chs], axis=1)
        in_maps.append({
            "x_prompt": xp, "x_sample": xs_, "cf": cfc, "cb": cbc, "norm1_w": n1, "gdn_norm_w": gnw,
            "w1": np.ascontiguousarray(w_in0[:, _w1_cols(g)])[None],
            "convw": np.ascontiguousarray(cwg)[None],
            "alog": np.ascontiguousarray(np.asarray(A_log, f)[0, 2 * g:2 * g + 2].reshape(1, 1, 2)),
            "dtb": np.ascontiguousarray(np.asarray(dt_bias, f)[0, 2 * g:2 * g + 2].reshape(1, 1, 2)),
            "cache_k": np.ascontiguousarray(np.asarray(cache_sb_k, f)[0][:, :, 2 * g:2 * g + 2, :])[None],
            "cache_v": np.ascontiguousarray(np.asarray(cache_sb_v, f)[0][:, :, 2 * g:2 * g + 2, :])[None],
            "st_S": np.ascontiguousarray(np.asarray(state_gdn_S, f)[0][:, 2 * g:2 * g + 2])[None],
            "st_conv": np.ascontiguousarray(stcg)[None],
        })
    r1 = run_bass_kernel_spmd(nc1, in_maps, core_ids=list(range(8))).results
    k_all = np.concatenate([r1[g]["o_k"][0] for g in range(8)], axis=1)
    v_all = np.concatenate([r1[g]["o_v"][0] for g in range(8)], axis=1)
    S_all = np.concatenate([r1[g]["o_S"][0] for g in range(8)], axis=1)
    conv_all = np.zeros((27, CONV_CH), f)
    for g in range(8):
        chs = [g * 128, 1024 + g * 128, 2048 + 2 * g * 128, 2048 + (2 * g + 1) * 128]
        for ci, c in enumerate(chs):
            conv_all[:, c:c + 128] = r1[g]["o_conv"][0][:, ci, :]
    oa_all = np.concatenate([r1[g]["o_oa"][0] for g in range(8)], axis=0)
    ob_all = np.concatenate([r1[g]["o_ob"][0] for g in range(8)], axis=0)
    key2 = ("p2", T8, PG)
    if key2 not in _CACHE:
        _CACHE[key2] = build_phase2(T8, PG)
    nc2 = _CACHE[key2]
    shared = {
        "cf": cfc, "cb": cbc,
        "nw3": np.ascontiguousarray(np.concatenate([np.asarray(norm1_w, f).reshape(32, 128),
                                                    np.asarray(norm2_w, f).reshape(32, 128),
                                                    np.asarray(final_norm_w, f).reshape(32, 128)], axis=0)),
        "w_gate": np.ascontiguousarray(w_in0[:, OFF_GATE:]),
        "w_gdn_o": np.asarray(w_gdn_o, f)[0], "w_sb_o": np.asarray(w_sb_o, f)[0],
        "w_out": np.asarray(w_out, f)[0], "w_up": np.asarray(w_up, f)[0], "w_down": np.asarray(w_down, f)[0],
    }
    in_maps2 = []
    for c in range(8):
        tok = np.concatenate([np.arange(c * T8, (c + 1) * T8), SEQ + 16 * c + np.arange(16)])
        xo = np.concatenate([xp[c * T8:(c + 1) * T8], xs_[16 * c:16 * c + 16]], axis=0)
        m = dict(shared)
        m["x_own"] = np.ascontiguousarray(xo)
        m["oaT"] = np.ascontiguousarray(oa_all[:, :, tok])
        m["obT"] = np.ascontiguousarray(ob_all[:, :, tok])
        in_maps2.append(m)
    r2 = run_bass_kernel_spmd(nc2, in_maps2, core_ids=list(range(8))).results
    y_prompt = np.concatenate([r2[c]["y"][:T8] for c in range(8)], axis=0)[None]
    y_sample = np.stack([r2[c]["y"][T8:] for c in range(8)], axis=0)
    return (y_prompt.astype(f), y_sample.astype(f),
            k_all[:SEQ][None, None], v_all[:SEQ][None, None],
            S_all[0][None, None], conv_all[0:3][None, None],
            k_all[SEQ:].reshape(8, 16, 16, 128)[None], v_all[SEQ:].reshape(8, 16, 16, 128)[None],
            S_all[1:][None], conv_all[3:].reshape(8, 3, CONV_CH)[None])
```
